# Optimizing a Trainium2 kernel written in Bass

```python
import jax
import jax.numpy as jnp
from jax import lax
import numpy as np

D_MODEL = 1024
BATCH = 2
SEQ = 8192
DEPTH = 1

MEM_LEN = 256
MOBA_HEADS = 8
MOBA_HEAD_DIM = 64
MOBA_BLOCK = 256
MOBA_TOPK = 3
FOX_HEADS = 8
FOX_HEAD_DIM = 64
MEM_HEADS = 4
MEM_HEAD_DIM = 128
Q_BLOCK = 128
ROPE_THETA = 500000.0
ROPE_FRACTION_DIV = 4
N_BRANCHES = 3
D_FF = ((-((-8 * D_MODEL) // 3)) + 255) // 256 * 256
NORM_EPS = 1e-6
MOBA_WIDTH = MOBA_HEADS * MOBA_HEAD_DIM
FOX_WIDTH = FOX_HEADS * FOX_HEAD_DIM
MEM_WIDTH = MEM_HEADS * MEM_HEAD_DIM
IN_SIZES = (MOBA_WIDTH,) * 3 + (FOX_WIDTH,) * 3 + (FOX_HEADS, MEM_WIDTH) + (D_MODEL,) * N_BRANCHES
D_IN = sum(IN_SIZES)

kernel_name = "hybrid_moba_fox_memory_layer"


def rms_norm(x, gain):
    xf = x.astype(jnp.float32)
    y = xf * lax.rsqrt(jnp.mean(xf * xf, axis=-1, keepdims=True) + NORM_EPS)
    return (y * gain.astype(jnp.float32)).astype(x.dtype)


def split_heads(t, n_heads):
    b, s, _ = t.shape
    return t.reshape(b, s, n_heads, -1).transpose(0, 2, 1, 3)


def merge_heads(t):
    b, h, s, d = t.shape
    return t.transpose(0, 2, 1, 3).reshape(b, s, h * d)


def partial_rope(t):
    seq, dim = t.shape[2], t.shape[3]
    rot = dim // ROPE_FRACTION_DIV
    half = rot // 2
    inv_freq = 1.0 / (ROPE_THETA ** (jnp.arange(half, dtype=jnp.float32) * 2.0 / rot))
    ang = jnp.arange(seq, dtype=jnp.float32)[:, None] * inv_freq[None, :]
    cos, sin = jnp.cos(ang), jnp.sin(ang)
    tf = t.astype(jnp.float32)
    x1, x2 = tf[..., :half], tf[..., half:rot]
    out = jnp.concatenate([x1 * cos - x2 * sin, x2 * cos + x1 * sin, tf[..., rot:]], axis=-1)
    return out.astype(t.dtype)


def moba_attention(q, k, v):
    b, h, s, d = q.shape
    scale = d ** -0.5
    n_blocks = -(-s // MOBA_BLOCK)
    pad = n_blocks * MOBA_BLOCK - s
    kp = jnp.pad(k, ((0, 0), (0, 0), (0, pad), (0, 0)))
    vp = jnp.pad(v, ((0, 0), (0, 0), (0, pad), (0, 0)))
    kb = kp.reshape(b, h, n_blocks, MOBA_BLOCK, d)
    vb = vp.reshape(b, h, n_blocks, MOBA_BLOCK, d)
    k_mean = jnp.mean(kb.astype(jnp.float32), axis=3)
    topk = min(MOBA_TOPK, n_blocks)
    b_idx = jnp.arange(b)[:, None, None, None]
    h_idx = jnp.arange(h)[None, :, None, None]
    key_off = jnp.arange(MOBA_BLOCK)
    q_off = jnp.arange(Q_BLOCK)
    block_ids = jnp.arange(n_blocks)

    def chunk(ci):
        t0 = ci * Q_BLOCK
        cur = t0 // MOBA_BLOCK
        qc = lax.dynamic_slice_in_dim(q, t0, Q_BLOCK, axis=2)
        gate = jnp.einsum('bhqd,bhnd->bhqn', qc.astype(jnp.float32), k_mean)
        gate = jnp.where(block_ids < cur, gate, -jnp.inf)
        _, sel = lax.top_k(gate, topk)
        sel_valid = sel < cur
        k_sel = kb[b_idx, h_idx, sel]
        v_sel = vb[b_idx, h_idx, sel]
        s_sel = jnp.einsum('bhqd,bhqnkd->bhqnk', qc, k_sel).astype(jnp.float32) * scale
        s_sel = jnp.where(sel_valid[..., None], s_sel, -jnp.inf)
        k_own = lax.dynamic_slice_in_dim(kp, cur * MOBA_BLOCK, MOBA_BLOCK, axis=2)
        v_own = lax.dynamic_slice_in_dim(vp, cur * MOBA_BLOCK, MOBA_BLOCK, axis=2)
        s_own = jnp.einsum('bhqd,bhkd->bhqk', qc, k_own).astype(jnp.float32) * scale
        causal = (key_off[None, :] + cur * MOBA_BLOCK) <= (q_off[:, None] + t0)
        s_own = jnp.where(causal, s_own, -jnp.inf)
        scores = jnp.concatenate([s_sel.reshape(b, h, Q_BLOCK, topk * MOBA_BLOCK), s_own], axis=-1)
        p = jax.nn.softmax(scores, axis=-1).astype(v.dtype)
        p_sel = p[..., :topk * MOBA_BLOCK].reshape(b, h, Q_BLOCK, topk, MOBA_BLOCK)
        p_own = p[..., topk * MOBA_BLOCK:]
        return (jnp.einsum('bhqnk,bhqnkd->bhqd', p_sel, v_sel)
                + jnp.einsum('bhqk,bhkd->bhqd', p_own, v_own))

    out = lax.map(chunk, jnp.arange(s // Q_BLOCK))
    return jnp.moveaxis(out, 0, 2).reshape(b, h, s, d)


def fox_attention(q, k, v, log_f):
    b, h, s, d = q.shape
    scale = d ** -0.5
    cum = jnp.cumsum(log_f, axis=-1)
    key_pos = jnp.arange(s)
    q_off = jnp.arange(Q_BLOCK)

    def chunk(ci):
        t0 = ci * Q_BLOCK
        qc = lax.dynamic_slice_in_dim(q, t0, Q_BLOCK, axis=2)
        cq = lax.dynamic_slice_in_dim(cum, t0, Q_BLOCK, axis=2)
        scores = (jnp.einsum('bhqd,bhkd->bhqk', qc, k).astype(jnp.float32) * scale
                  + cq[..., None] - cum[:, :, None, :])
        causal = key_pos[None, :] <= (q_off[:, None] + t0)
        scores = jnp.where(causal, scores, -jnp.inf)
        p = jax.nn.softmax(scores, axis=-1).astype(v.dtype)
        return jnp.einsum('bhqk,bhkd->bhqd', p, v)

    out = lax.map(chunk, jnp.arange(s // Q_BLOCK))
    return jnp.moveaxis(out, 0, 2).reshape(b, h, s, d)


def memory_attention(q, mk, mv):
    scale = q.shape[-1] ** -0.5
    scores = jnp.einsum('bhsd,bhmd->bhsm', q, mk).astype(jnp.float32) * scale
    p = jax.nn.softmax(scores, axis=-1).astype(mv.dtype)
    return jnp.einsum('bhsm,bhmd->bhsd', p, mv)


def setup_inputs(seed: int = 0) -> dict:
    key = jax.random.key(seed)
    ks = jax.random.split(key, 24)
    L = DEPTH

    def normal(k, shape, scale):
        return jax.random.normal(k, shape, jnp.float32) * scale

    def gain(k, shape):
        return 1.0 + 0.1 * jax.random.normal(k, shape, jnp.float32)

    return {
        "x": normal(ks[0], (BATCH, SEQ, D_MODEL), 1.0),
        "mem": normal(ks[1], (BATCH, MEM_LEN, D_MODEL), 1.0),
        "g_mix": gain(ks[2], (L, D_MODEL)),
        "w_in": normal(ks[3], (L, D_MODEL, D_IN), D_MODEL ** -0.5),
        "b_forget": 4.0 + 0.5 * jax.random.normal(ks[4], (L, FOX_HEADS), jnp.float32),
        "g_q_moba": gain(ks[5], (L, MOBA_HEAD_DIM)),
        "g_k_moba": gain(ks[6], (L, MOBA_HEAD_DIM)),
        "g_q_fox": gain(ks[7], (L, FOX_HEAD_DIM)),
        "g_k_fox": gain(ks[8], (L, FOX_HEAD_DIM)),
        "g_q_mem": gain(ks[9], (L, MEM_HEAD_DIM)),
        "g_k_mem": gain(ks[10], (L, MEM_HEAD_DIM)),
        "g_mem": gain(ks[11], (L, D_MODEL)),
        "w_mem_kv": normal(ks[12], (L, D_MODEL, 2 * MEM_WIDTH), D_MODEL ** -0.5),
        "w_br_moba": normal(ks[13], (L, MOBA_WIDTH, D_MODEL), MOBA_WIDTH ** -0.5),
        "w_br_fox": normal(ks[14], (L, FOX_WIDTH, D_MODEL), FOX_WIDTH ** -0.5),
        "w_br_mem": normal(ks[15], (L, MEM_WIDTH, D_MODEL), MEM_WIDTH ** -0.5),
        "w_out": normal(ks[16], (L, D_MODEL, D_MODEL), D_MODEL ** -0.5),
        "g_ffn": gain(ks[17], (L, D_MODEL)),
        "w_gate": normal(ks[18], (L, D_MODEL, D_FF), D_MODEL ** -0.5),
        "w_up": normal(ks[19], (L, D_MODEL, D_FF), D_MODEL ** -0.5),
        "w_down": normal(ks[20], (L, D_FF, D_MODEL), D_FF ** -0.5),
    }


def reference(x, mem, g_mix, w_in, b_forget, g_q_moba, g_k_moba, g_q_fox, g_k_fox, g_q_mem, g_k_mem,
              g_mem, w_mem_kv, w_br_moba, w_br_fox, w_br_mem, w_out, g_ffn, w_gate, w_up, w_down):
    split_points = np.cumsum(IN_SIZES)[:-1].tolist()
    for layer in range(DEPTH):
        h = rms_norm(x, g_mix[layer])
        proj = h @ w_in[layer]
        (q_m, k_m, v_m, q_f, k_f, v_f, f_logit, q_c, a_m, a_f, a_c) = jnp.split(proj, split_points, axis=-1)

        q_m = partial_rope(rms_norm(split_heads(q_m, MOBA_HEADS), g_q_moba[layer]))
        k_m = partial_rope(rms_norm(split_heads(k_m, MOBA_HEADS), g_k_moba[layer]))
        y_m = moba_attention(q_m, k_m, split_heads(v_m, MOBA_HEADS))

        log_f = jax.nn.log_sigmoid(f_logit.astype(jnp.float32)
                                   + b_forget[layer].astype(jnp.float32)).transpose(0, 2, 1)
        q_f = rms_norm(split_heads(q_f, FOX_HEADS), g_q_fox[layer])
        k_f = rms_norm(split_heads(k_f, FOX_HEADS), g_k_fox[layer])
        y_f = fox_attention(q_f, k_f, split_heads(v_f, FOX_HEADS), log_f)

        mkv = rms_norm(mem, g_mem[layer]) @ w_mem_kv[layer]
        mk, mv = jnp.split(mkv, 2, axis=-1)
        mk = rms_norm(split_heads(mk, MEM_HEADS), g_k_mem[layer])
        q_c = rms_norm(split_heads(q_c, MEM_HEADS), g_q_mem[layer])
        y_c = memory_attention(q_c, mk, split_heads(mv, MEM_HEADS))

        merged = (jax.nn.sigmoid(a_m) * (merge_heads(y_m) @ w_br_moba[layer])
                  + jax.nn.sigmoid(a_f) * (merge_heads(y_f) @ w_br_fox[layer])
                  + jax.nn.sigmoid(a_c) * (merge_heads(y_c) @ w_br_mem[layer]))
        x = x + merged @ w_out[layer]

        h2 = rms_norm(x, g_ffn[layer])
        x = x + (jax.nn.silu(h2 @ w_gate[layer]) * (h2 @ w_up[layer])) @ w_down[layer]
    return x
```

```python
import contextlib
import os
import numpy as np
import ml_dtypes
import concourse.bass as bass
import concourse.mybir as mybir
from concourse.bass_utils import run_bass_kernel_spmd

F32 = mybir.dt.float32
BF16 = mybir.dt.bfloat16
ALU = mybir.AluOpType
AF = mybir.ActivationFunctionType
AX = mybir.AxisListType

D = 1024
KC = 8
EPS = 1e-6
BIG = 30000.0
NBF = ml_dtypes.bfloat16


class Sched:
    ENG = ("pe", "act", "dve", "pool", "sp")
    EMAP = {"pe": "tensor", "act": "scalar", "dve": "vector", "pool": "gpsimd", "sp": "sync"}

    def __init__(self, nc):
        self.nc = nc
        self.ops = []
        self.cnt = {}
        self.lastw = {}
        self.readers = {}
        self.seen = {e: {} for e in self.ENG}
        self.last_on = {}
        self.limit = int(os.environ["KLIMIT"]) if os.environ.get("KLIMIT") else None
        self.marks = []

    def add(self, eng, fn, reads=(), writes=(), dma=None, small=False):
        idx = len(self.ops)
        if self.limit is not None and idx >= self.limit:
            return
        deps = set()
        for k in list(reads) + list(writes):
            if k in self.lastw:
                deps.add(self.lastw[k])
        for k in writes:
            deps.update(self.readers.get(k, ()))
        waits = []
        for d in sorted(deps):
            o = self.ops[d]
            same = o["dma"] is None and dma is None and o["eng"] == eng
            if same and eng != "pool" and (eng == "pe" or (not o["small"] and d != self.last_on.get(eng))):
                continue
            sem, val = o["done"]
            if self.seen[eng].get(sem, 0) >= val:
                continue
            self.seen[eng][sem] = val
            waits.append((sem, val))
        if dma is not None:
            sem, inc = "d_" + dma, 16
        else:
            sem, inc = "e_" + eng, 1
        self.cnt[sem] = self.cnt.get(sem, 0) + inc
        self.ops.append(dict(eng=eng, fn=fn, waits=waits, sig=(sem, inc), done=(sem, self.cnt[sem]),
                             dma=dma, small=small))
        self.last_on[eng] = idx
        for k in writes:
            self.lastw[k] = idx
            self.readers[k] = []
        for k in reads:
            self.readers.setdefault(k, []).append(idx)

    def emit(self, stack):
        sems = {}
        for name in self.cnt:
            sems[name] = stack.enter_context(self.nc.semaphore(name))
        with self.nc.Block() as block:
            for e in self.ENG:
                ops = [o for o in self.ops if o["eng"] == e]

                def body(eng, ops=ops):
                    for o in ops:
                        for sem, val in o["waits"]:
                            eng.wait_ge(sems[sem], val)
                        ins = o["fn"](eng)
                        if ins is not None:
                            ins.then_inc(sems[o["sig"][0]], o["sig"][1])

                getattr(block, self.EMAP[e])(body)


def _mk_ident(S, ident_f, ident_b):
    S.add("pool", lambda g: g.memset(ident_f[:], 1.0), writes=[("identf",)])
    S.add("pool", lambda g: g.affine_select(out=ident_f[:], in_=ident_f[:], pattern=[[-1, 128]],
                                            compare_op=ALU.is_equal, fill=0.0, base=0, channel_multiplier=1),
          reads=[("identf",)], writes=[("identf",)])
    S.add("pool", lambda g: g.tensor_copy(out=ident_b[:], in_=ident_f[:]), reads=[("identf",)], writes=[("ident",)])


def build_mlp(NTOK=2048):
    NT = NTOK // 128
    DFF = 2816
    NF = DFF // 128
    nc = bass.Bass("TRN2", target_bir_lowering=False)
    x = nc.dram_tensor("x", [NTOK, D], F32, kind="ExternalInput").ap()
    yT = nc.dram_tensor("yT", [3, 512, NTOK], BF16, kind="ExternalInput").ap()
    wg = nc.dram_tensor("wg", [D, 3072], F32, kind="ExternalInput").ap()
    wbr = nc.dram_tensor("wbr", [3, 512, D], F32, kind="ExternalInput").ap()
    wout = nc.dram_tensor("wout", [D, D], F32, kind="ExternalInput").ap()
    gmix = nc.dram_tensor("gmix", [D], F32, kind="ExternalInput").ap()
    gffn = nc.dram_tensor("gffn", [D], F32, kind="ExternalInput").ap()
    wgate = nc.dram_tensor("wgate", [D, DFF], F32, kind="ExternalInput").ap()
    wup = nc.dram_tensor("wup", [D, DFF], F32, kind="ExternalInput").ap()
    wdown = nc.dram_tensor("wdown", [DFF, D], F32, kind="ExternalInput").ap()
    out = nc.dram_tensor("out", [NTOK, D], F32, kind="ExternalOutput").ap()
    x1s = nc.dram_tensor("x1s", [NTOK, D], F32, kind="Internal").ap()

    with contextlib.ExitStack() as st:
        def sb(name, shape, dt):
            return st.enter_context(nc.sbuf_tensor(name, shape, dt))

        def ps(name, shape, dt):
            return st.enter_context(nc.psum_tensor(name, shape, dt))

        XA = sb("XA", [128, 36864], BF16)
        YA = sb("YA", [128, 3 * 4 * NTOK], BF16)
        Z = sb("Z", [128, KC, NTOK], BF16)
        xt = sb("xt", [128, 2, D], F32)
        gb = sb("gb", [128, D], F32)
        h = sb("h", [128, D], BF16)
        hT = sb("hT", [128, KC, 128], BF16)
        sig = sb("sig", [128, 3072], BF16)
        merged = sb("merged", [128, D], F32)
        tmp = sb("tmp", [128, 512], F32)
        mb = sb("mb", [128, D], BF16)
        junk = sb("junk", [128, D], BF16)
        stat = sb("stat", [128, 3 * 2 * NT + 8], F32)
        sgt = sb("sgt", [128, 2, 512], F32)
        ident_f = sb("identf", [128, 128], F32)
        ident = sb("ident", [128, 128], BF16)
        tp = ps("tp", [128, D], BF16)
        pss = [ps("ps%d" % i, [128, 512], F32) for i in range(6)]

        Wg = XA[:, 0:24576].rearrange("p (k c) -> p k c", k=KC)
        Wbr = XA[:, 24576:36864].rearrange("p (b k c) -> p b k c", b=3, k=4)
        Wout = XA[:, 0:8192].rearrange("p (k c) -> p k c", k=KC)
        Wd = XA[:, 0:NF * D].rearrange("p (f c) -> p f c", f=NF)
        Wst = XA[:, NF * D:NF * D + 4096].rearrange("p (s w k c) -> p s w k c", s=2, w=2, k=KC)
        yTs = YA[:, :].rearrange("p (b k t) -> p b k t", b=3, k=4)
        actT = YA[:, 0:NF * 1024].rearrange("p (f t) -> p f t", f=NF)

        S = Sched(nc)
        _mk_ident(S, ident_f, ident)
        S.add("dve", lambda v: v.memset(stat[:], 0.0), writes=[("stat",)])
        for k in range(KC):
            S.add("pool", lambda g, k=k: g.dma_start(out=Wg[:, k, :], in_=wg[k * 128:(k + 1) * 128, :]),
                  writes=[("Wg", k)], dma="Wg%d" % k)
        for b in range(3):
            S.add("pool", lambda g, b=b: g.dma_start(
                out=Wbr[:, b, :, :], in_=wbr[b].rearrange("(k p) c -> p k c", p=128)),
                writes=[("Wbr", b)], dma="Wbr%d" % b)
            S.add("pool", lambda g, b=b: g.dma_start(
                out=yTs[:, b, :, :], in_=yT[b].rearrange("(k p) t -> p k t", p=128)),
                writes=[("yT", b)], dma="yT%d" % b)
        S.add("sp", lambda e: e.dma_start(out=gb[:], in_=gmix.partition_broadcast(128)),
              writes=[("gb",)], dma="gb")

        scol = [0]

        def norm_to_T(b, dstT, dst_key, extra_writes=()):
            c0 = scol[0]
            scol[0] += 3
            S.add("dve", lambda v: v.scalar_tensor_tensor(
                out=junk[:], in0=xt[:, b, :], scalar=1.0, in1=xt[:, b, :], op0=ALU.mult, op1=ALU.mult,
                accum_out=stat[:, c0:c0 + 1]), reads=[("xt", b), ("stat",)], writes=[("junk",), ("st", c0)],
                small=True)
            S.add("act", lambda a: a.activation(out=stat[:, c0 + 1:c0 + 2], in_=stat[:, c0:c0 + 1], func=AF.Ln,
                                                scale=1.0 / D, bias=EPS),
                  reads=[("st", c0)], writes=[("st", c0 + 1)], small=True)
            S.add("act", lambda a: a.activation(out=stat[:, c0 + 2:c0 + 3], in_=stat[:, c0 + 1:c0 + 2],
                                                func=AF.Exp, scale=-0.5),
                  reads=[("st", c0 + 1)], writes=[("st", c0 + 2)], small=True)
            S.add("dve", lambda v: v.scalar_tensor_tensor(
                out=h[:], in0=xt[:, b, :], scalar=stat[:, c0 + 2:c0 + 3], in1=gb[:], op0=ALU.mult,
                op1=ALU.mult), reads=[("xt", b), ("st", c0 + 2), ("gb",)], writes=[("h",)])

            def tr(t):
                ins = None
                for k in range(KC):
                    ins = t.transpose(tp[:, k * 128:(k + 1) * 128], h[:, k * 128:(k + 1) * 128], ident[:])
                return ins
            S.add("pe", tr, reads=[("h",), ("ident",)], writes=[("tp",)])
            S.add("act", lambda a: a.copy(out=dstT, in_=tp[:, :].rearrange("p (k c) -> p k c", k=KC)),
                  reads=[("tp",)], writes=[dst_key] + list(extra_writes))

        pcnt = [0]

        def nextps():
            i = pcnt[0] % len(pss)
            pcnt[0] += 1
            return i

        for t in range(NT):
            b = t % 2
            tc0 = t * 128
            S.add("sp", lambda e, t=t, b=b: e.dma_start(out=xt[:, b, :], in_=x[t * 128:(t + 1) * 128, :]),
                  writes=[("xt", b)], dma="xt%d" % b)
            norm_to_T(b, hT[:, :, :], ("hT",))
            for gi in range(6):
                pi = nextps()

                def mm(te, gi=gi, pi=pi):
                    ins = None
                    for k in range(KC):
                        ins = te.matmul(pss[pi][:, :], lhsT=hT[:, k, :], rhs=Wg[:, k, gi * 512:(gi + 1) * 512],
                                        start=(k == 0), stop=(k == KC - 1))
                    return ins
                S.add("pe", mm, reads=[("hT",)] + [("Wg", k) for k in range(KC)], writes=[("ps", pi)])
                S.add("act", lambda a, gi=gi, pi=pi: a.activation(
                    out=sig[:, gi * 512:(gi + 1) * 512], in_=pss[pi][:, :], func=AF.Sigmoid),
                    reads=[("ps", pi)], writes=[("sig", gi)])
            for br in range(3):
                for hf in range(2):
                    pi = nextps()

                    def mm(te, br=br, hf=hf, pi=pi, tc0=tc0):
                        ins = None
                        for k in range(4):
                            ins = te.matmul(pss[pi][:, :], lhsT=yTs[:, br, k, tc0:tc0 + 128],
                                            rhs=Wbr[:, br, k, hf * 512:(hf + 1) * 512],
                                            start=(k == 0), stop=(k == 3))
                        return ins
                    S.add("pe", mm, reads=[("yT", br), ("Wbr", br)], writes=[("ps", pi)])
                    gi = br * 2 + hf
                    if br == 0:
                        S.add("dve", lambda v, hf=hf, pi=pi, gi=gi: v.tensor_tensor(
                            out=merged[:, hf * 512:(hf + 1) * 512], in0=pss[pi][:, :],
                            in1=sig[:, gi * 512:(gi + 1) * 512], op=ALU.mult),
                            reads=[("ps", pi), ("sig", gi)], writes=[("merged", hf)])
                    else:
                        S.add("dve", lambda v, pi=pi, gi=gi: v.tensor_tensor(
                            out=tmp[:, :], in0=pss[pi][:, :], in1=sig[:, gi * 512:(gi + 1) * 512], op=ALU.mult),
                            reads=[("ps", pi), ("sig", gi)], writes=[("tmp",)])
                        S.add("dve", lambda v, hf=hf: v.tensor_tensor(
                            out=merged[:, hf * 512:(hf + 1) * 512], in0=merged[:, hf * 512:(hf + 1) * 512],
                            in1=tmp[:, :], op=ALU.add), reads=[("tmp",), ("merged", hf)], writes=[("merged", hf)])
            S.add("dve", lambda v: v.tensor_copy(out=mb[:], in_=merged[:]),
                  reads=[("merged", 0), ("merged", 1)], writes=[("mb",)])

            def tr(te):
                ins = None
                for k in range(KC):
                    ins = te.transpose(tp[:, k * 128:(k + 1) * 128], mb[:, k * 128:(k + 1) * 128], ident[:])
                return ins
            S.add("pe", tr, reads=[("mb",), ("ident",)], writes=[("tp",)])
            S.add("act", lambda a, tc0=tc0: a.copy(out=Z[:, :, tc0:tc0 + 128],
                                                   in_=tp[:, :].rearrange("p (k c) -> p k c", k=KC)),
                  reads=[("tp",)], writes=[("Z", t)])

        allA = [("Wg", k) for k in range(KC)] + [("Wbr", b) for b in range(3)]
        S.add("pool", lambda g: g.dma_start(out=Wout[:, :, :], in_=wout.rearrange("(k p) c -> p k c", p=128)),
              writes=[("Wout",)] + allA, dma="Wout")
        S.add("sp", lambda e: e.dma_start(out=gb[:], in_=gffn.partition_broadcast(128)),
              writes=[("gb",)], dma="gb")
        for t in range(NT):
            b = t % 2
            tc0 = t * 128
            S.add("sp", lambda e, t=t, b=b: e.dma_start(out=xt[:, b, :], in_=x[t * 128:(t + 1) * 128, :]),
                  writes=[("xt", b)], dma="xt%d" % b)
            for hf in range(2):
                pi = nextps()

                def mm(te, hf=hf, pi=pi, tc0=tc0):
                    ins = None
                    for k in range(KC):
                        ins = te.matmul(pss[pi][:, :], lhsT=Z[:, k, tc0:tc0 + 128],
                                        rhs=Wout[:, k, hf * 512:(hf + 1) * 512], start=(k == 0), stop=(k == KC - 1))
                    return ins
                S.add("pe", mm, reads=[("Z", t), ("Wout",)], writes=[("ps", pi)])
                S.add("dve", lambda v, hf=hf, pi=pi, b=b: v.tensor_tensor(
                    out=xt[:, b, hf * 512:(hf + 1) * 512], in0=pss[pi][:, :], in1=xt[:, b, hf * 512:(hf + 1) * 512],
                    op=ALU.add), reads=[("ps", pi), ("xt", b)], writes=[("xt", b)])
            S.add("sp", lambda e, t=t, b=b: e.dma_start(out=x1s[t * 128:(t + 1) * 128, :], in_=xt[:, b, :]),
                  reads=[("xt", b)], writes=[("x1s", t)], dma="x1w%d" % t)
            norm_to_T(b, Z[:, :, tc0:tc0 + 128], ("Z", t))

        S.add("pool", lambda g: g.dma_start(out=Wd[:, 0:11, :],
                                            in_=wdown[0:11 * 128, :].rearrange("(f p) c -> p f c", p=128)),
              writes=[("Wd", 0), ("Wout",)] + allA, dma="Wd0")
        S.add("pool", lambda g: g.dma_start(out=Wd[:, 11:22, :],
                                            in_=wdown[11 * 128:22 * 128, :].rearrange("(f p) c -> p f c", p=128)),
              writes=[("Wd", 1)], dma="Wd1")
        first_act = [True]
        for half in range(NTOK // 1024):
            for f in range(NF):
                sbi = f % 2
                S.add("pool", lambda g, f=f, sbi=sbi: g.dma_start(
                    out=Wst[:, sbi, 0, :, :], in_=wgate[:, f * 128:(f + 1) * 128].rearrange("(k p) c -> p k c", p=128)),
                    writes=[("Wst", sbi, 0)] + ([("Wout",)] + allA if (half == 0 and f < 2) else []),
                    dma="Wst%d_0" % sbi)
                S.add("pool", lambda g, f=f, sbi=sbi: g.dma_start(
                    out=Wst[:, sbi, 1, :, :], in_=wup[:, f * 128:(f + 1) * 128].rearrange("(k p) c -> p k c", p=128)),
                    writes=[("Wst", sbi, 1)], dma="Wst%d_1" % sbi)
                for sl in range(2):
                    c0 = half * 1024 + sl * 512
                    pa, pb = nextps(), nextps()

                    def mm(te, sbi=sbi, c0=c0, pa=pa, pb=pb):
                        ins = None
                        for w, pi in ((0, pa), (1, pb)):
                            for k in range(KC):
                                ins = te.matmul(pss[pi][:, :], lhsT=Wst[:, sbi, w, k, :], rhs=Z[:, k, c0:c0 + 512],
                                                start=(k == 0), stop=(k == KC - 1))
                        return ins
                    S.add("pe", mm, reads=[("Wst", sbi, 0), ("Wst", sbi, 1)] + [("Z", c0 // 128 + i) for i in range(4)],
                          writes=[("ps", pa), ("ps", pb)])
                    S.add("act", lambda a, sl=sl, pa=pa: a.activation(out=sgt[:, sl, :], in_=pss[pa][:, :], func=AF.Silu),
                          reads=[("ps", pa)], writes=[("sgt", sl)])
                    S.add("dve", lambda v, f=f, sl=sl, pb=pb: v.tensor_tensor(
                        out=actT[:, f, sl * 512:(sl + 1) * 512], in0=pss[pb][:, :], in1=sgt[:, sl, :], op=ALU.mult),
                        reads=[("ps", pb), ("sgt", sl)],
                        writes=[("actT", f, sl)] + ([("yT", bb) for bb in range(3)] if first_act[0] else []))
                    first_act[0] = False
            for tt in range(8):
                t = half * 8 + tt
                b = t % 2
                S.add("sp", lambda e, t=t, b=b: e.dma_start(out=xt[:, b, :], in_=x1s[t * 128:(t + 1) * 128, :]),
                      reads=[("x1s", t)], writes=[("xt", b)], dma="xt%d" % b)
                for hf in range(2):
                    pi = nextps()

                    def mm(te, tt=tt, hf=hf, pi=pi):
                        ins = None
                        for f in range(NF):
                            ins = te.matmul(pss[pi][:, :], lhsT=actT[:, f, tt * 128:(tt + 1) * 128],
                                            rhs=Wd[:, f, hf * 512:(hf + 1) * 512], start=(f == 0), stop=(f == NF - 1))
                        return ins
                    S.add("pe", mm, reads=[("Wd", 0), ("Wd", 1)] + [("actT", f, tt // 4) for f in range(NF)],
                          writes=[("ps", pi)])
                    S.add("dve", lambda v, hf=hf, pi=pi, b=b: v.tensor_tensor(
                        out=xt[:, b, hf * 512:(hf + 1) * 512], in0=pss[pi][:, :], in1=xt[:, b, hf * 512:(hf + 1) * 512],
                        op=ALU.add), reads=[("ps", pi), ("xt", b)], writes=[("xt", b)])
                S.add("sp", lambda e, t=t, b=b: e.dma_start(out=out[t * 128:(t + 1) * 128, :], in_=xt[:, b, :]),
                      reads=[("xt", b)], writes=[("out", t)], dma="outw")
        S.add("sp", lambda e: None, reads=[("out", t) for t in range(NT)])
        S.emit(st)
    return nc


_CACHE = {}


def _run_mlp(x_flat, yT_all, inp):
    if "mlp" not in _CACHE:
        _CACHE["mlp"] = build_mlp()
    nc = _CACHE["mlp"]
    L = 0
    w_in = inp["w_in"][L]
    common = dict(
        wg=np.ascontiguousarray(w_in[:, 3592:6664]),
        wbr=np.ascontiguousarray(np.stack([inp["w_br_moba"][L], inp["w_br_fox"][L], inp["w_br_mem"][L]])),
        wout=np.ascontiguousarray(inp["w_out"][L]), gmix=np.ascontiguousarray(inp["g_mix"][L]),
        gffn=np.ascontiguousarray(inp["g_ffn"][L]), wgate=np.ascontiguousarray(inp["w_gate"][L]),
        wup=np.ascontiguousarray(inp["w_up"][L]), wdown=np.ascontiguousarray(inp["w_down"][L]))
    in_maps = []
    for c in range(8):
        m = dict(common)
        m["x"] = np.ascontiguousarray(x_flat[c * 2048:(c + 1) * 2048])
        m["yT"] = np.ascontiguousarray(yT_all[:, :, c * 2048:(c + 1) * 2048])
        in_maps.append(m)
    res = run_bass_kernel_spmd(nc, in_maps, core_ids=list(range(8)))
    return np.concatenate([np.asarray(r["out"]) for r in res.results], axis=0)


def build_attn(S_=8192):
    NT = S_ // 128
    NCH = S_ // 512
    NW = 898
    THETA_LN8 = float(np.log(500000.0) / 8.0)
    nc = bass.Bass("TRN2", target_bir_lowering=False)
    x = nc.dram_tensor("x", [S_, D], F32, kind="ExternalInput").ap()
    mem = nc.dram_tensor("mem", [256, D], F32, kind="ExternalInput").ap()
    wsel = nc.dram_tensor("wsel", [D, NW], F32, kind="ExternalInput").ap()
    wmem = nc.dram_tensor("wmem", [D, 256], F32, kind="ExternalInput").ap()
    gmix = nc.dram_tensor("gmix", [D], F32, kind="ExternalInput").ap()
    gmem = nc.dram_tensor("gmem", [D], F32, kind="ExternalInput").ap()
    gqk = nc.dram_tensor("gqk", [512], F32, kind="ExternalInput").ap()
    gqc = nc.dram_tensor("gqc", [128], F32, kind="ExternalInput").ap()
    gkc = nc.dram_tensor("gkc", [128], F32, kind="ExternalInput").ap()
    bfg = nc.dram_tensor("bfg", [2], F32, kind="ExternalInput").ap()
    yT = nc.dram_tensor("yT", [3, 128, S_], BF16, kind="ExternalOutput").ap()

    with contextlib.ExitStack() as st:
        def sb(name, shape, dt):
            return st.enter_context(nc.sbuf_tensor(name, shape, dt))

        def ps(name, shape, dt):
            return st.enter_context(nc.psum_tensor(name, shape, dt))

        Ws = sb("Ws", [128, KC, NW], BF16)
        Wm = sb("Wm", [128, KC, 256], BF16)
        KTm = [sb("KTm%d" % i, [128, S_], BF16) for i in range(2)]
        KTf = sb("KTf", [128, S_], BF16)
        V = {"m": sb("Vm", [128, NT, 192], BF16), "f": sb("Vf", [128, NT, 192], BF16)}
        QTm = [sb("QTm%d" % i, [128, 512], BF16) for i in range(2)]
        QTf = sb("QTf", [128, 512], BF16)
        qcT = sb("qcT", [128, 512], BF16)
        mkT = sb("mkT", [128, 256], BF16)
        mvS = sb("mvS", [128, 2, 128], BF16)
        xt = sb("xt", [128, 2, D], F32)
        gb = sb("gb", [128, D], F32)
        gqk_b = sb("gqk_b", [128, 512], F32)
        gqc_b = sb("gqc_b", [128, 128], F32)
        gkc_b = sb("gkc_b", [128, 128], F32)
        bf_b = sb("bf_b", [128, 2], F32)
        h = sb("h", [128, D], BF16)
        hT = sb("hT", [128, KC, 128], BF16)
        junk = sb("junk", [128, D], BF16)
        sqt = sb("sqt", [128, 512], F32)
        t32 = sb("t32", [128, 512], F32)
        kn32 = sb("kn32", [128, 512], F32)
        knb = sb("knb", [128, 512], BF16)
        qcn = sb("qcn", [128, 128], BF16)
        rp = sb("rp", [128, 4, 32], F32)
        NSTAT = 34 * NT + 64
        stat = sb("stat", [128, NSTAT], F32)
        posi = sb("posi", [128, NT], mybir.dt.int32)
        posf = sb("posf", [128, NT], F32)
        fi = sb("fi", [128, 8], mybir.dt.int32)
        invf = sb("invf", [128, 8], F32)
        ang = sb("ang", [128, NT, 8], F32)
        rnd = sb("rnd", [128, NT, 8], F32)
        sin_t = sb("sin_t", [128, NT, 8], F32)
        cos_t = sb("cos_t", [128, NT, 8], F32)
        ident_f = sb("identf", [128, 128], F32)
        ident = sb("ident", [128, 128], BF16)
        triU = sb("triU", [128, 128], F32)
        onesf = sb("onesf", [128, 128], F32)
        ones_b = sb("ones_b", [128, 128], BF16)
        sel127 = sb("sel127", [128, 128], F32)
        trif = sb("trif", [128, 128], F32)
        trib = sb("trib", [128, 128], BF16)
        oh = sb("oh", [32, 2048], BF16)
        lft = sb("lft", [128, 8], F32)
        lfc = sb("lfc", [128, 4, 2], F32)
        cumL = sb("cumL", [128, NT, 2], F32)
        Cb = sb("Cb", [128, 2], F32)
        biasf = sb("biasf", [128, 2, NT], F32)
        kmsum = sb("kmsum", [128, 2, 32], F32)
        kmT = [sb("kmT%d" % i, [128, 32], BF16) for i in range(2)]
        gsb = sb("gsb", [128, 4, 32], F32)
        mx = sb("mx", [128, 4, 8], F32)
        selb = sb("selb", [128, 4, 32], F32)
        mbias = sb("mbias", [128, 4, 32], F32)
        mbb = sb("mbb", [128, 4, 32], BF16)
        PT = [sb("PT%d" % i, [128, 512], BF16) for i in range(4)]
        rec = sb("rec", [128, 512], F32)
        yst = {"m": sb("ystm", [128, 512], BF16), "f": sb("ystf", [128, 512], BF16), "c": sb("ystc", [128, 512], BF16)}

        tp = ps("tp", [128, D], BF16)
        ps1 = ps("ps1", [128, 512], F32)
        ps2 = ps("ps2", [128, 512], F32)
        tpq = ps("tpq", [128, 1024], BF16)
        Sps = [ps("S%d" % i, [128, 512], F32) for i in range(2)]
        Ops = ps("Ops", [128, 512], F32)
        misc = ps("misc", [128, 512], F32)

        S = Sched(nc)
        A = S.add
        _mk_ident(S, ident_f, ident)

        def P(fn, r=(), w=()):
            A("pool", fn, reads=list(r), writes=list(w))
        P(lambda g: g.memset(triU[:], 1.0), w=[("triU",)])
        P(lambda g: g.memset(sel127[:], 1.0), w=[("sel127",)])
        P(lambda g: g.memset(trif[:], 0.0), w=[("trif",)])
        P(lambda g: g.iota(posi[:], pattern=[[128, NT]], base=0, channel_multiplier=1), w=[("posi",)])
        P(lambda g: g.iota(fi[:], pattern=[[1, 8]], base=0, channel_multiplier=0), w=[("fi",)])
        P(lambda g: g.affine_select(out=triU[:], in_=triU[:], pattern=[[1, 128]], compare_op=ALU.is_ge, fill=0.0,
                                    base=0, channel_multiplier=-1), r=[("triU",)], w=[("triU",)])
        P(lambda g: g.affine_select(out=sel127[:], in_=sel127[:], pattern=[[0, 128]], compare_op=ALU.is_ge, fill=0.0,
                                    base=-127, channel_multiplier=1), r=[("sel127",)], w=[("sel127",)])
        P(lambda g: g.affine_select(out=trif[:], in_=trif[:], pattern=[[1, 128]], compare_op=ALU.is_ge, fill=-BIG,
                                    base=0, channel_multiplier=-1), r=[("trif",)], w=[("trif",)])
        P(lambda g: g.tensor_copy(out=posf[:], in_=posi[:]), r=[("posi",)], w=[("posf",)])
        P(lambda g: g.tensor_copy(out=invf[:], in_=fi[:]), r=[("fi",)], w=[("invf0",)])
        P(lambda g: g.memset(onesf[:], 1.0), w=[("onesf",)])
        P(lambda g: g.memset(ones_b[:], 1.0), w=[("ones_b",)])
        P(lambda g: g.tensor_copy(out=trib[:], in_=trif[:]), r=[("trif",)], w=[("trib",)])
        P(lambda g: g.memset(Cb[:], 0.0), r=[("triU",), ("sel127",), ("posf",), ("onesf",), ("ones_b",), ("trib",), ("invf0",)],
          w=[("consts",)])
        for qd in range(S_ // 2048):
            P(lambda g: g.memset(oh[:], 1.0), w=[("oh",)])
            P(lambda g, qd=qd: g.affine_select(out=oh[:], in_=oh[:], pattern=[[1, 2048]], compare_op=ALU.is_ge, fill=0.0,
                                               base=2048 * qd, channel_multiplier=-256), r=[("oh",)], w=[("oh",)])
            P(lambda g, qd=qd: g.affine_select(out=oh[:], in_=oh[:], pattern=[[-1, 2048]], compare_op=ALU.is_ge, fill=0.0,
                                               base=255 - 2048 * qd, channel_multiplier=256), r=[("oh",)], w=[("oh",)])
            for i in range(2):
                A("dve", lambda v, i=i, qd=qd: v.tensor_copy(out=KTm[i][64:96, qd * 2048:(qd + 1) * 2048], in_=oh[:]),
                  reads=[("oh",)], writes=[("KTm", i, "oh", qd)])

        def vinit(v):
            v.memset(stat[:], 0.0)
            v.memset(V["m"][:, :, 64:128], 1.0)
            v.memset(V["f"][:, :, 64:128], 1.0)
            v.memset(kmT[0][:], 0.0)
            v.memset(kmT[1][:], 0.0)
            return v.memset(kmsum[:], 0.0)
        A("dve", vinit, writes=[("stat",), ("Vones",), ("kmT", 0), ("kmT", 1), ("kmsum",)])
        A("act", lambda a: a.activation(out=invf[:], in_=invf[:], func=AF.Exp, scale=-THETA_LN8),
          reads=[("consts",), ("invf0",)], writes=[("invf",)], small=True)
        A("dve", lambda v: v.tensor_tensor(out=ang[:], in0=posf[:, :].unsqueeze(2).broadcast_to([128, NT, 8]),
                                           in1=invf[:, :].unsqueeze(1).broadcast_to([128, NT, 8]), op=ALU.mult),
          reads=[("consts",), ("invf",)], writes=[("ang",)])
        for name, off, dst in (("sin", 0.0, sin_t), ("cos", 0.25, cos_t)):
            A("dve", lambda v, off=off: v.tensor_scalar(out=rnd[:], in0=ang[:], scalar1=float(1.0 / (2 * np.pi)), scalar2=off,
                                                        op0=ALU.mult, op1=ALU.add), reads=[("ang",)], writes=[("rnd",)])
            A("dve", lambda v: v.tensor_scalar(out=sqt[:, 0:NT * 8].rearrange("p (t f) -> p t f", f=8), in0=rnd[:],
                                               scalar1=12582912.0, scalar2=12582912.0, op0=ALU.add, op1=ALU.subtract),
              reads=[("rnd",)], writes=[("sq",)])
            A("dve", lambda v: v.tensor_tensor(out=rnd[:], in0=rnd[:],
                                               in1=sqt[:, 0:NT * 8].rearrange("p (t f) -> p t f", f=8),
                                               op=ALU.subtract), reads=[("rnd",), ("sq",)], writes=[("rnd",)])
            A("act", lambda a, dst=dst: a.activation(out=dst[:], in_=rnd[:], func=AF.Sin, scale=float(2 * np.pi)),
              reads=[("rnd",)], writes=[(name,)])
        A("sp", lambda e: e.dma_start(out=gqk_b[:], in_=gqk.partition_broadcast(128)), writes=[("gqk",)], dma="gqk")
        A("sp", lambda e: e.dma_start(out=gqc_b[:], in_=gqc.partition_broadcast(128)), writes=[("gqc",)], dma="gqc")
        A("sp", lambda e: e.dma_start(out=gkc_b[:], in_=gkc.partition_broadcast(128)), writes=[("gkc",)], dma="gkc")
        A("sp", lambda e: e.dma_start(out=bf_b[:], in_=bfg.partition_broadcast(128)), writes=[("bf",)], dma="bf")
        A("sp", lambda e: e.dma_start(out=gb[:], in_=gmem.partition_broadcast(128)), writes=[("gb",)], dma="gb")

        def gsc(v):
            v.tensor_scalar(out=gqk_b[:, 0:128], in0=gqk_b[:, 0:128], scalar1=0.125, scalar2=None, op0=ALU.mult)
            v.tensor_scalar(out=gqk_b[:, 256:384], in0=gqk_b[:, 256:384], scalar1=0.125, scalar2=None, op0=ALU.mult)
            return v.tensor_scalar(out=gqc_b[:], in0=gqc_b[:], scalar1=float(128 ** -0.5), scalar2=None, op0=ALU.mult)
        A("dve", gsc, reads=[("gqk",), ("gqc",)], writes=[("gqk",), ("gqc",)])
        for k in range(KC):
            A("pool", lambda g, k=k: g.dma_start(out=Ws[:, k, :], in_=wsel[k * 128:(k + 1) * 128, :]),
              writes=[("Ws", k)], dma="Ws%d" % k)
        A("pool", lambda g: g.dma_start(out=Wm[:, :, :], in_=wmem.rearrange("(k p) c -> p k c", p=128)),
          writes=[("Wm",)], dma="Wm")
        WsK = [("Ws", k) for k in range(KC)]

        scol = [0]

        def cols(n):
            c = scol[0]
            scol[0] += n
            assert scol[0] <= NSTAT
            return c

        def norm_to_hT(b):
            c0 = cols(3)
            A("dve", lambda v: v.scalar_tensor_tensor(
                out=junk[:], in0=xt[:, b, :], scalar=1.0, in1=xt[:, b, :], op0=ALU.mult, op1=ALU.mult,
                accum_out=stat[:, c0:c0 + 1]), reads=[("xt", b), ("stat",)], writes=[("junk",), ("st", c0)], small=True)
            A("act", lambda a: a.activation(out=stat[:, c0 + 1:c0 + 2], in_=stat[:, c0:c0 + 1], func=AF.Ln,
                                            scale=1.0 / D, bias=EPS), reads=[("st", c0)], writes=[("st", c0 + 1)], small=True)
            A("act", lambda a: a.activation(out=stat[:, c0 + 2:c0 + 3], in_=stat[:, c0 + 1:c0 + 2], func=AF.Exp,
                                            scale=-0.5), reads=[("st", c0 + 1)], writes=[("st", c0 + 2)], small=True)
            A("dve", lambda v: v.scalar_tensor_tensor(
                out=h[:], in0=xt[:, b, :], scalar=stat[:, c0 + 2:c0 + 3], in1=gb[:], op0=ALU.mult, op1=ALU.mult),
              reads=[("xt", b), ("st", c0 + 2), ("gb",)], writes=[("h",)])

            def tr(t):
                ins = None
                for k in range(KC):
                    ins = t.transpose(tp[:, k * 128:(k + 1) * 128], h[:, k * 128:(k + 1) * 128], ident[:])
                return ins
            A("pe", tr, reads=[("h",), ("ident",)], writes=[("tp",)])
            A("act", lambda a: a.copy(out=hT[:, :, :], in_=tp[:, :].rearrange("p (k c) -> p k c", k=KC)),
              reads=[("tp",)], writes=[("hT",)])

        def qknorm(src, srckey, W, nh, hd, gain, gkey, dst, dstkey, rope_T=None):
            c0 = cols(3 * nh)
            A("act", lambda a: a.activation(out=sqt[:, 0:W], in_=src, func=AF.Square), reads=[srckey], writes=[("sq",)])
            A("dve", lambda v: v.tensor_reduce(out=stat[:, c0:c0 + nh], in_=sqt[:, 0:W].rearrange("p (h d) -> p h d", h=nh),
                                               axis=AX.X, op=ALU.add), reads=[("sq",), ("stat",)], writes=[("st", c0)], small=True)
            A("act", lambda a: a.activation(out=stat[:, c0 + nh:c0 + 2 * nh], in_=stat[:, c0:c0 + nh], func=AF.Ln,
                                            scale=1.0 / hd, bias=EPS), reads=[("st", c0)], writes=[("st", c0 + 1)], small=True)
            A("act", lambda a: a.activation(out=stat[:, c0 + 2 * nh:c0 + 3 * nh], in_=stat[:, c0 + nh:c0 + 2 * nh],
                                            func=AF.Exp, scale=-0.5), reads=[("st", c0 + 1)], writes=[("st", c0 + 2)], small=True)
            A("dve", lambda v: v.tensor_tensor(
                out=t32[:, 0:W].rearrange("p (h d) -> p h d", h=nh), in0=src.rearrange("p (h d) -> p h d", h=nh),
                in1=stat[:, c0 + 2 * nh:c0 + 3 * nh].unsqueeze(2).broadcast_to([128, nh, hd]), op=ALU.mult),
              reads=[srckey, ("st", c0 + 2)], writes=[("t32",)])
            if rope_T is None:
                A("dve", lambda v: v.tensor_tensor(out=dst, in0=t32[:, 0:W], in1=gain, op=ALU.mult),
                  reads=[("t32",), gkey], writes=[dstkey])
                return
            T = rope_T

            A("dve", lambda v: v.tensor_tensor(out=kn32[:, 0:W], in0=t32[:, 0:W], in1=gain, op=ALU.mult),
              reads=[("t32",), gkey], writes=[("kn32",)])

            def f(v):
                v.tensor_copy(out=dst, in_=kn32[:, 0:W])
                kv = kn32[:, 0:256].rearrange("p (h d) -> p h d", h=4)
                cb = cos_t[:, T, :].unsqueeze(1).broadcast_to([128, 4, 8])
                sb_ = sin_t[:, T, :].unsqueeze(1).broadcast_to([128, 4, 8])
                v.tensor_tensor(out=rp[:, :, 0:8], in0=kv[:, :, 0:8], in1=cb, op=ALU.mult)
                v.tensor_tensor(out=rp[:, :, 8:16], in0=kv[:, :, 8:16], in1=sb_, op=ALU.mult)
                v.tensor_tensor(out=rp[:, :, 16:24], in0=kv[:, :, 8:16], in1=cb, op=ALU.mult)
                return v.tensor_tensor(out=rp[:, :, 24:32], in0=kv[:, :, 0:8], in1=sb_, op=ALU.mult)
            A("dve", f, reads=[("kn32",), ("sin",), ("cos",)], writes=[dstkey, ("rp",)], small=True)

            def f2(v):
                dv = dst[:, 0:256].rearrange("p (h d) -> p h d", h=4)
                v.tensor_tensor(out=dv[:, :, 0:8], in0=rp[:, :, 0:8], in1=rp[:, :, 8:16], op=ALU.subtract)
                return v.tensor_tensor(out=dv[:, :, 8:16], in0=rp[:, :, 16:24], in1=rp[:, :, 24:32], op=ALU.add)
            A("dve", f2, reads=[("rp",), dstkey], writes=[dstkey], small=True)

        S.marks.append(("setup_end", len(S.ops)))
        for mt in range(2):
            b = mt % 2
            A("sp", lambda e, mt=mt, b=b: e.dma_start(out=xt[:, b, :], in_=mem[mt * 128:(mt + 1) * 128, :]),
              writes=[("xt", b)], dma="xt%d" % b)
            norm_to_hT(b)

            def mm(te):
                ins = None
                for k in range(KC):
                    ins = te.matmul(ps1[:, 0:256], lhsT=hT[:, k, :], rhs=Wm[:, k, :], start=(k == 0), stop=(k == KC - 1))
                return ins
            A("pe", mm, reads=[("hT",), ("Wm",)], writes=[("ps1",)])
            A("act", lambda a, mt=mt: a.copy(out=mvS[:, mt, :], in_=ps1[:, 128:256]), reads=[("ps1",)], writes=[("mvS", mt)])
            qknorm(ps1[:, 0:128], ("ps1",), 128, 1, 128, gkc_b[:], ("gkc",), qcn[:], ("qcn",))
            A("pe", lambda te: te.transpose(tpq[:, 0:128], qcn[:], ident[:]), reads=[("qcn",), ("ident",)], writes=[("tpq",)])
            A("act", lambda a, mt=mt: a.copy(out=mkT[:, mt * 128:(mt + 1) * 128], in_=tpq[:, 0:128]),
              reads=[("tpq",)], writes=[("mkT", mt)])
        A("sp", lambda e: e.dma_start(out=gb[:], in_=gmix.partition_broadcast(128)), writes=[("gb",)], dma="gb")

        S.marks.append(("memkv_end", len(S.ops)))
        scnt = [0]
        pcnt = [0]

        def attn_head(c, kind, e):
            for kc in range(c + 1):
                for i in range(4):
                    T = 4 * kc + i
                    diag = kc == c
                    q0 = 128 * i if diag else 0
                    si = scnt[0] % 2
                    scnt[0] += 1
                    pi = pcnt[0] % 4
                    pcnt[0] += 1
                    if kind == "m":
                        lhsT = KTm[e][0:96, T * 128:(T + 1) * 128]
                        rhs = QTm[e][0:96, q0:512]
                        rk = [("KTm", e, T), ("KTm", e, "oh", T // 16), ("QTm", e, "q"), ("QTm", e, "aug")]
                    else:
                        lhsT = KTf[e * 64:(e + 1) * 64, T * 128:(T + 1) * 128]
                        rhs = QTf[e * 64:(e + 1) * 64, q0:512]
                        rk = [("KTf", T), ("QTf",)]

                    def mm(te, lhsT=lhsT, rhs=rhs, si=si, q0=q0, diag=diag):
                        ins = te.matmul(Sps[si][:, q0:512], lhsT=lhsT, rhs=rhs, start=True, stop=not diag,
                                        skip_group_check=True)
                        if diag:
                            ins = te.matmul(Sps[si][:, q0:q0 + 128], lhsT=ident[:], rhs=trib[:], start=False, stop=True,
                                            skip_group_check=True)
                        return ins
                    A("pe", mm, reads=rk + [("ident",), ("consts",)], writes=[("S", si)])
                    if kind == "f":
                        A("act", lambda a, si=si, pi=pi, q0=q0, T=T: a.activation(
                            out=PT[pi][:, q0:512], in_=Sps[si][:, q0:512], func=AF.Exp, bias=biasf[:, e, T:T + 1]),
                          reads=[("S", si), ("biasf", e)], writes=[("PT", pi)])
                    else:
                        A("act", lambda a, si=si, pi=pi, q0=q0: a.activation(
                            out=PT[pi][:, q0:512], in_=Sps[si][:, q0:512], func=AF.Exp),
                          reads=[("S", si)], writes=[("PT", pi)])
                    A("pe", lambda te, pi=pi, q0=q0, T=T, diag=diag, i=i: te.matmul(
                        Ops[:, q0:512], lhsT=V[kind][:, T, e * 64:e * 64 + 128], rhs=PT[pi][:, q0:512],
                        start=(T == 0), stop=(diag and i == 3), skip_group_check=True),
                      reads=[("PT", pi), ("V", kind, T), ("Vones",)], writes=[("O",)])
            yr = slice(e * 64, (e + 1) * 64)
            sr = slice((1 - e) * 64, (2 - e) * 64)
            A("dve", lambda v: v.reciprocal(out=rec[sr, :], in_=Ops[sr, :]), reads=[("O",)], writes=[("rec",)])
            A("dve", lambda v: v.tensor_tensor(out=yst[kind][yr, :], in0=Ops[yr, :], in1=rec[sr, :], op=ALU.mult),
              reads=[("O",), ("rec",)], writes=[("yst", kind, e)])

        for c in range(NCH):
            for i in range(4):
                T = 4 * c + i
                b = T % 2
                A("sp", lambda e, T=T, b=b: e.dma_start(out=xt[:, b, :], in_=x[T * 128:(T + 1) * 128, :]),
                  writes=[("xt", b)], dma="xt%d" % b)
                norm_to_hT(b)

                def mm1(te):
                    ins = None
                    for k in range(KC):
                        ins = te.matmul(ps1[:, :], lhsT=hT[:, k, :], rhs=Ws[:, k, 0:512], start=(k == 0), stop=(k == KC - 1))
                    return ins
                A("pe", mm1, reads=[("hT",)] + WsK, writes=[("ps1",)])

                def mm2(te):
                    ins = None
                    for k in range(KC):
                        ins = te.matmul(ps2[:, 0:386], lhsT=hT[:, k, :], rhs=Ws[:, k, 512:898], start=(k == 0), stop=(k == KC - 1))
                    return ins
                A("pe", mm2, reads=[("hT",)] + WsK, writes=[("ps2",)])
                qknorm(ps1[:, :], ("ps1",), 512, 8, 64, gqk_b[:], ("gqk",), knb[:], ("knb",), rope_T=T)

                def trq(te):
                    ins = None
                    for j in range(4):
                        ins = te.transpose(tpq[:, j * 128:(j + 1) * 128], knb[:, j * 128:(j + 1) * 128], ident[:])
                    return ins
                A("pe", trq, reads=[("knb",), ("ident",)], writes=[("tpq",)])
                ic = slice(i * 128, (i + 1) * 128)
                Tc = slice(T * 128, (T + 1) * 128)
                A("dve", lambda a, ic=ic: a.tensor_copy(out=QTm[0][0:64, ic], in_=tpq[0:64, 0:128]),
                  reads=[("tpq",)], writes=[("QTm", 0, "q")])
                A("dve", lambda v, ic=ic: v.tensor_copy(out=QTm[1][0:64, ic], in_=tpq[64:128, 0:128]),
                  reads=[("tpq",)], writes=[("QTm", 1, "q")])
                A("dve", lambda a, Tc=Tc: a.tensor_copy(out=KTm[0][0:64, Tc], in_=tpq[0:64, 128:256]),
                  reads=[("tpq",)], writes=[("KTm", 0, T)])
                A("dve", lambda v, Tc=Tc: v.tensor_copy(out=KTm[1][0:64, Tc], in_=tpq[64:128, 128:256]),
                  reads=[("tpq",)], writes=[("KTm", 1, T)])
                A("dve", lambda v, ic=ic: v.tensor_copy(out=QTf[:, ic], in_=tpq[:, 256:384]), reads=[("tpq",)], writes=[("QTf",)])
                A("dve", lambda a, Tc=Tc: a.tensor_copy(out=KTf[:, Tc], in_=tpq[:, 384:512]), reads=[("tpq",)], writes=[("KTf", T)])
                def vcp(a, T=T):
                    a.copy(out=V["m"][:, T, 0:64], in_=ps2[:, 0:64])
                    a.copy(out=V["m"][:, T, 128:192], in_=ps2[:, 64:128])
                    a.copy(out=V["f"][:, T, 0:64], in_=ps2[:, 128:192])
                    return a.copy(out=V["f"][:, T, 128:192], in_=ps2[:, 192:256])
                A("act", vcp, reads=[("ps2",), ("Vones",)], writes=[("V", "m", T), ("V", "f", T)])
                A("dve", lambda v: v.tensor_tensor(out=lft[:, 0:2], in0=ps2[:, 384:386], in1=bf_b[:], op=ALU.add),
                  reads=[("ps2",), ("bf",)], writes=[("lft", 0)], small=True)
                A("act", lambda a: a.activation(out=lft[:, 2:4], in_=lft[:, 0:2], func=AF.Exp, scale=-1.0),
                  reads=[("lft", 0)], writes=[("lft", 1)], small=True)
                A("act", lambda a, i=i: a.activation(out=lfc[:, i, :], in_=lft[:, 2:4], func=AF.Ln, bias=1.0),
                  reads=[("lft", 1)], writes=[("lfc", i)], small=True)
                qknorm(ps2[:, 256:384], ("ps2",), 128, 1, 128, gqc_b[:], ("gqc",), qcn[:], ("qcn",))
                A("pe", lambda te: te.transpose(tpq[:, 512:640], qcn[:], ident[:]), reads=[("qcn",), ("ident",)], writes=[("tpq",)])
                A("dve", lambda a, ic=ic: a.tensor_copy(out=qcT[:, ic], in_=tpq[:, 512:640]), reads=[("tpq",)], writes=[("qcT",)])
                if i % 2 == 1:
                    blk = T // 2
                    for e in range(2):
                        A("dve", lambda v, e=e, blk=blk: v.tensor_reduce(
                            out=kmsum[0:64, e, blk:blk + 1], in_=KTm[e][0:64, blk * 256:(blk + 1) * 256], axis=AX.X, op=ALU.add),
                          reads=[("KTm", e, T - 1), ("KTm", e, T), ("kmsum",)], writes=[("kms", e, blk)], small=True)
                        A("act", lambda a, e=e, blk=blk: a.activation(out=kmT[e][0:64, blk:blk + 1], in_=kmsum[0:64, e, blk:blk + 1],
                                                                      func=AF.Identity, scale=1.0 / 256),
                          reads=[("kms", e, blk)], writes=[("kmT", e)], small=True)
            S.marks.append(("phaseA_end_c%d" % c, len(S.ops)))
            def cs(te):
                ins = None
                for i in range(4):
                    ins = te.matmul(misc[:, 128 + 2 * i:130 + 2 * i], lhsT=triU[:], rhs=lfc[:, i, :], start=True, stop=(i == 0),
                                    skip_group_check=True)
                    for j in range(i):
                        ins = te.matmul(misc[:, 128 + 2 * i:130 + 2 * i], lhsT=onesf[:], rhs=lfc[:, j, :], start=False,
                                        stop=(j == i - 1), skip_group_check=True)
                return ins
            A("pe", cs, reads=[("lfc", i) for i in range(4)] + [("consts",)], writes=[("misc",)])
            A("dve", lambda v, c=c: v.tensor_tensor(
                out=cumL[:, 4 * c:4 * c + 4, :], in0=misc[:, 128:136].rearrange("p (i e) -> p i e", e=2),
                in1=Cb[:, :].unsqueeze(1).broadcast_to([128, 4, 2]), op=ALU.add),
              reads=[("misc",), ("Cb",)], writes=[("cumL", c)], small=True)
            A("pe", lambda te, c=c: te.matmul(misc[:, 136:138], lhsT=sel127[:], rhs=cumL[:, 4 * c + 3, :], start=True, stop=True),
              reads=[("cumL", c), ("consts",)], writes=[("misc",)])
            A("act", lambda a: a.copy(out=Cb[:], in_=misc[:, 136:138]), reads=[("misc",)], writes=[("Cb",)], small=True)
            for e in range(2):
                A("dve", lambda v, e=e, c=c: v.tensor_scalar(
                    out=biasf[:, e, 0:4 * c + 4], in0=cumL[:, 0:4 * c + 4, e], scalar1=Cb[:, e:e + 1], scalar2=None,
                    op0=ALU.subtract), reads=[("Cb",)] + [("cumL", cc) for cc in range(c + 1)], writes=[("biasf", e)], small=True)
            S.marks.append(("cumsum_end_c%d" % c, len(S.ops)))
            for e in range(2):
                def gm(te, e=e):
                    ins = None
                    for i in range(4):
                        ins = te.matmul(misc[:, i * 32:(i + 1) * 32], lhsT=QTm[e][0:64, i * 128:(i + 1) * 128], rhs=kmT[e][0:64, :],
                                        start=True, stop=True)
                    return ins
                A("pe", gm, reads=[("QTm", e, "q"), ("kmT", e)], writes=[("misc",)])

                A("dve", lambda v: v.tensor_copy(out=gsb[:], in_=misc[:, 0:128].rearrange("p (i n) -> p i n", n=32)),
                  reads=[("misc",)], writes=[("gsb",)])

                def g1(v, c=c):
                    v.memset(gsb[:, 0:2, 2 * c:32], -BIG)
                    return v.memset(gsb[:, 2:4, 2 * c + 1:32], -BIG)
                A("dve", g1, reads=[("gsb",)], writes=[("gsb",)], small=True)

                def g2(v):
                    ins = None
                    for i in range(4):
                        ins = v.max(out=mx[:, i, :], in_=gsb[:, i, :])
                    return ins
                A("dve", g2, reads=[("gsb",)], writes=[("mx",)], small=True)
                A("dve", lambda v: v.tensor_tensor(out=selb[:], in0=gsb[:], in1=mx[:, :, 2:3].broadcast_to([128, 4, 32]),
                                                   op=ALU.is_ge), reads=[("gsb",), ("mx",)], writes=[("selb",)], small=True)
                A("dve", lambda v: v.tensor_scalar(out=mbias[:], in0=selb[:], scalar1=BIG, scalar2=-BIG, op0=ALU.mult,
                                                   op1=ALU.add), reads=[("selb",)], writes=[("mbias",)], small=True)

                def g3(v, c=c):
                    v.memset(mbias[:, 0:2, 2 * c:32], -BIG)
                    return v.memset(mbias[:, 2:4, 2 * c + 1:32], -BIG)
                A("dve", g3, reads=[("mbias",)], writes=[("mbias",)], small=True)

                def g4(v, c=c):
                    v.memset(mbias[:, 0:2, 2 * c:2 * c + 1], 0.0)
                    return v.memset(mbias[:, 2:4, 2 * c + 1:2 * c + 2], 0.0)
                A("dve", g4, reads=[("mbias",)], writes=[("mbias",)], small=True)
                A("dve", lambda v: v.tensor_copy(out=mbb[:], in_=mbias[:]), reads=[("mbias",)], writes=[("mbb",)], small=True)

                def gt(te):
                    ins = None
                    for i in range(4):
                        ins = te.transpose(tp[0:32, i * 128:(i + 1) * 128], mbb[:, i, :], ident[:])
                    return ins
                A("pe", gt, reads=[("mbb",), ("ident",)], writes=[("tp",)])
                A("dve", lambda v, e=e: v.tensor_copy(out=QTm[e][64:96, :], in_=tp[0:32, 0:512]),
                  reads=[("tp",)], writes=[("QTm", e, "aug")])
            S.marks.append(("gating_end_c%d" % c, len(S.ops)))
            for kind, oi in (("m", 0), ("f", 1)):
                for e in range(2):
                    attn_head(c, kind, e)
                A("sp", lambda en, kind=kind, oi=oi, c=c: en.dma_start(out=yT[oi, :, c * 512:(c + 1) * 512], in_=yst[kind][:, :]),
                  reads=[("yst", kind, 0), ("yst", kind, 1)], writes=[("yout", oi, c)], dma="yo%d" % oi)
            S.marks.append(("attn_end_c%d" % c, len(S.ops)))
            for mt in range(2):
                si = scnt[0] % 2
                scnt[0] += 1
                pi = pcnt[0] % 4
                pcnt[0] += 1
                A("pe", lambda te, mt=mt, si=si: te.matmul(Sps[si][:, :], lhsT=mkT[:, mt * 128:(mt + 1) * 128], rhs=qcT[:, :],
                                                          start=True, stop=True),
                  reads=[("mkT", mt), ("qcT",)], writes=[("S", si)])
                A("act", lambda a, si=si, pi=pi: a.activation(out=PT[pi][:, :], in_=Sps[si][:, :], func=AF.Exp),
                  reads=[("S", si)], writes=[("PT", pi)])

                def pv(te, mt=mt, pi=pi):
                    te.matmul(Ops[:, :], lhsT=mvS[:, mt, :], rhs=PT[pi][:, :], start=(mt == 0), stop=(mt == 1))
                    return te.matmul(ps2[:, :], lhsT=ones_b[:], rhs=PT[pi][:, :], start=(mt == 0), stop=(mt == 1))
                A("pe", pv, reads=[("PT", pi), ("mvS", mt), ("consts",)], writes=[("O",), ("ps2",)])
            A("dve", lambda v: v.reciprocal(out=rec[:, :], in_=ps2[:, :]), reads=[("ps2",)], writes=[("rec",)])
            A("dve", lambda v: v.tensor_tensor(out=yst["c"][:, :], in0=Ops[:, :], in1=rec[:, :], op=ALU.mult),
              reads=[("O",), ("rec",)], writes=[("yst", "c", 0)])
            A("sp", lambda en, c=c: en.dma_start(out=yT[2, :, c * 512:(c + 1) * 512], in_=yst["c"][:, :]),
              reads=[("yst", "c", 0)], writes=[("yout", 2, c)], dma="yo2")
        S.marks.append(("end", len(S.ops)))
        A("sp", lambda en: None, reads=[("yout", o, c) for o in range(3) for c in range(NCH)])
        if os.environ.get("KMARKS"):
            print(S.marks[:12])
        S.emit(st)
    return nc


def _run_attn(inp):
    if "attn" not in _CACHE:
        _CACHE["attn"] = build_attn()
    nc = _CACHE["attn"]
    L = 0
    w_in = inp["w_in"][L]
    wkv = inp["w_mem_kv"][L]
    S_ = inp["x"].shape[1]
    in_maps = []
    for c in range(8):
        b, g = c // 4, c % 4
        sl = slice(g * 128, (g + 1) * 128)
        segs = [w_in[:, 0:512][:, sl], w_in[:, 512:1024][:, sl], w_in[:, 1536:2048][:, sl], w_in[:, 2048:2560][:, sl],
                w_in[:, 1024:1536][:, sl], w_in[:, 2560:3072][:, sl], w_in[:, 3080:3592][:, sl],
                w_in[:, 3072 + 2 * g:3072 + 2 * g + 2]]
        gq_m, gk_m = inp["g_q_moba"][L], inp["g_k_moba"][L]
        gq_f, gk_f = inp["g_q_fox"][L], inp["g_k_fox"][L]
        in_maps.append(dict(
            x=np.ascontiguousarray(inp["x"][b]), mem=np.ascontiguousarray(inp["mem"][b]),
            wsel=np.ascontiguousarray(np.concatenate(segs, axis=1)),
            wmem=np.ascontiguousarray(np.concatenate([wkv[:, 0:512][:, sl], wkv[:, 512:1024][:, sl]], axis=1)),
            gmix=np.ascontiguousarray(inp["g_mix"][L]), gmem=np.ascontiguousarray(inp["g_mem"][L]),
            gqk=np.ascontiguousarray(np.concatenate([gq_m, gq_m, gk_m, gk_m, gq_f, gq_f, gk_f, gk_f])),
            gqc=np.ascontiguousarray(inp["g_q_mem"][L]), gkc=np.ascontiguousarray(inp["g_k_mem"][L]),
            bfg=np.ascontiguousarray(inp["b_forget"][L][2 * g:2 * g + 2])))
    res = run_bass_kernel_spmd(nc, in_maps, core_ids=list(range(8)))
    yT_all = np.zeros((3, 512, 2 * S_), dtype=NBF)
    for c in range(8):
        b, g = c // 4, c % 4
        y = np.asarray(res.results[c]["yT"])
        yT_all[:, g * 128:(g + 1) * 128, b * S_:(b + 1) * S_] = y
    return yT_all


def kernel(**inputs):
    inp = {k: np.asarray(v) for k, v in inputs.items()}
    yT_all = _run_attn(inp)
    x_flat = np.ascontiguousarray(inp["x"].reshape(-1, D))
    out = _run_mlp(x_flat, yT_all, inp)
    return out.reshape(inp["x"].shape).astype(np.float32)
```
